# Optimizing a Trainium2 kernel written in Bass

```python
import math
import jax, jax.numpy as jnp
from jax import lax
import numpy as np

D_MODEL = 1024
BATCH = 8
SEQ = 4096
DEPTH = 1

D_CONV = D_MODEL
CONV_WIDTH = 3
N_HEADS = 16
HEAD_DIM = 64
D_ATTN = N_HEADS * HEAD_DIM
ATTN_PATTERNS = ((128, 1), (512, 4), (2048, 16))
QB = 128
IN_COLS = 4 * D_CONV + 4 * D_ATTN + 2 * D_MODEL
EPS = 1e-6

kernel_name = "hybrid_gated_shortconv_dilated_alibi_attn"


def rms_norm(x, g):
    xf = x.astype(jnp.float32)
    y = xf * lax.rsqrt(jnp.mean(xf * xf, axis=-1, keepdims=True) + EPS)
    return (y * g.astype(jnp.float32)).astype(x.dtype)


def alibi_slopes(n_heads):
    return jnp.exp2(-8.0 * jnp.arange(1, n_heads + 1, dtype=jnp.float32) / n_heads)


def causal_depthwise_conv(u, w):
    return lax.conv_general_dilated(
        u, w.astype(u.dtype)[:, None, :], window_strides=(1,),
        padding=[(CONV_WIDTH - 1, 0)],
        dimension_numbers=("NWC", "WIO", "NWC"),
        feature_group_count=u.shape[-1])


def dilated_window_attention(q, k, v, window, dilation, slopes):
    B, S, H, Dh = q.shape
    d = dilation
    win = window // dilation
    assert win == QB
    L = S // d
    nb = -(-L // QB)
    Lp = nb * QB

    def to_sub(t):
        t = t.reshape(B, L, d, H, Dh)
        return jnp.pad(t, ((0, 0), (0, Lp - L), (0, 0), (0, 0), (0, 0)))

    qs = (to_sub(q) * (Dh ** -0.5)).reshape(B, nb, QB, d, H, Dh)

    def key_blocks(t):
        tp = jnp.pad(to_sub(t), ((0, 0), (QB, 0), (0, 0), (0, 0), (0, 0)))
        prev = tp[:, :Lp].reshape(B, nb, QB, d, H, Dh)
        cur = tp[:, QB:QB + Lp].reshape(B, nb, QB, d, H, Dh)
        return jnp.concatenate([prev, cur], axis=2)

    kb = key_blocks(k)
    vb = key_blocks(v)

    s = jnp.einsum("bnqrhe,bnkrhe->bnrhqk", qs, kb,
                   preferred_element_type=jnp.float32)
    q_loc = jnp.arange(QB)[:, None] + QB
    k_loc = jnp.arange(2 * QB)[None, :]
    delta = q_loc - k_loc
    key_sub = jnp.arange(nb)[:, None] * QB - QB + jnp.arange(2 * QB)[None, :]
    valid = ((delta >= 0) & (delta <= win))[None] & (key_sub >= 0)[:, None, :]
    bias = -slopes[:, None, None] * (d * delta).astype(jnp.float32)[None]
    s = jnp.where(valid[None, :, None, None], s + bias, -jnp.inf)

    m = jnp.max(s, axis=-1, keepdims=True)
    p = jnp.exp(s - m)
    den = jnp.sum(p, axis=-1)
    lse = m[..., 0] + jnp.log(den)
    o = jnp.einsum("bnrhqk,bnkrhe->bnqrhe", p, vb.astype(jnp.float32))
    o = o / jnp.transpose(den, (0, 1, 4, 2, 3))[..., None]
    o = o.reshape(B, Lp, d, H, Dh)[:, :L].reshape(B, S, H, Dh)
    lse = jnp.transpose(lse, (0, 1, 4, 2, 3)).reshape(B, Lp, d, H)[:, :L].reshape(B, S, H)
    return o, lse


def mixture_of_dilations(q, k, v, slopes):
    outs, lses = [], []
    for window, dilation in ATTN_PATTERNS:
        o, lse = dilated_window_attention(q, k, v, window, dilation, slopes)
        outs.append(o)
        lses.append(lse)
    alpha = jax.nn.softmax(jnp.stack(lses, axis=0), axis=0)
    o = jnp.sum(alpha[..., None] * jnp.stack(outs, axis=0), axis=0)
    return o.astype(q.dtype)


def hybrid_mixer(h, w_in, b_merge, conv_w, w_out_conv, w_out_attn, w_o):
    B, S, _ = h.shape
    proj = jnp.einsum("bsd,dc->bsc", h, w_in)
    splits = np.cumsum([D_CONV] * 4 + [D_ATTN] * 4 + [D_MODEL])
    xc, bg, cg, zc, q, k, v, za, g_conv, g_attn = jnp.split(proj, splits, axis=-1)

    c = causal_depthwise_conv(cg * xc, conv_w)
    y_conv = jnp.einsum("bsc,cd->bsd", jax.nn.silu(zc) * bg * c, w_out_conv)

    shp = (B, S, N_HEADS, HEAD_DIM)
    o = mixture_of_dilations(q.reshape(shp), k.reshape(shp), v.reshape(shp),
                             alibi_slopes(N_HEADS)).reshape(B, S, D_ATTN)
    y_attn = jnp.einsum("bsc,cd->bsd", jax.nn.silu(za) * o, w_out_attn)

    g_conv = jax.nn.sigmoid(g_conv + b_merge[:D_MODEL])
    g_attn = jax.nn.sigmoid(g_attn + b_merge[D_MODEL:])
    merged = g_conv * y_conv + g_attn * y_attn
    return jnp.einsum("bsd,de->bse", merged, w_o)


def setup_inputs(seed: int = 0) -> dict:
    key = jax.random.key(seed)
    ks = jax.random.split(key, 10)
    f32 = jnp.float32
    x = jax.random.normal(ks[0], (BATCH, SEQ, D_MODEL), f32)
    norm_g = 1.0 + 0.02 * jax.random.normal(ks[1], (DEPTH, D_MODEL), f32)
    w_in = jax.random.normal(ks[2], (DEPTH, D_MODEL, IN_COLS), f32) * D_MODEL ** -0.5
    b_merge = 0.01 * jax.random.normal(ks[3], (DEPTH, 2 * D_MODEL), f32)
    conv_w = jax.random.normal(ks[4], (DEPTH, CONV_WIDTH, D_CONV), f32) * CONV_WIDTH ** -0.5
    w_out_conv = jax.random.normal(ks[5], (DEPTH, D_CONV, D_MODEL), f32) * D_CONV ** -0.5
    w_out_attn = jax.random.normal(ks[6], (DEPTH, D_ATTN, D_MODEL), f32) * D_ATTN ** -0.5
    w_o = jax.random.normal(ks[7], (DEPTH, D_MODEL, D_MODEL), f32) * D_MODEL ** -0.5
    final_g = 1.0 + 0.02 * jax.random.normal(ks[8], (D_MODEL,), f32)
    return {"x": x, "norm_g": norm_g, "w_in": w_in, "b_merge": b_merge,
            "conv_w": conv_w, "w_out_conv": w_out_conv, "w_out_attn": w_out_attn,
            "w_o": w_o, "final_g": final_g}


def reference(x, norm_g, w_in, b_merge, conv_w, w_out_conv, w_out_attn, w_o, final_g):
    h = x
    for layer in range(DEPTH):
        u = rms_norm(h, norm_g[layer])
        h = h + hybrid_mixer(u, w_in[layer], b_merge[layer], conv_w[layer],
                             w_out_conv[layer], w_out_attn[layer], w_o[layer])
    return rms_norm(h, final_g)
```

```python
import os
import numpy as np
import ml_dtypes
import concourse.bass as bass
import concourse.mybir as mybir
from concourse.bass_utils import run_bass_kernel_spmd
from contextlib import ExitStack

F32 = mybir.dt.float32
BF16 = mybir.dt.bfloat16
AF = mybir.ActivationFunctionType
ALU = mybir.AluOpType

SEQ = 4096
DM = 1024
NCH = 8
IN_COLS = 10240
EPS = 1e-6
PATTERNS = (16, 4, 1)
NEG = -30000.0
_STOP = int(os.environ.get("KSTOP", "9"))
_SUB = int(os.environ.get("KSUB", "9"))
_NPAT = int(os.environ.get("KPAT", "3"))
_NOEW = int(os.environ.get("KNOEW", "0"))
_KLN = int(os.environ.get("KLN", "0"))
_KP2 = int(os.environ.get("KP2", "1"))
_KPIPE = int(os.environ.get("KPIPE", "3"))
_KDRAIN = int(os.environ.get("KDRAIN", "0"))
_KBLK = int(os.environ.get("KBLK", "2"))
_KEXP = int(os.environ.get("KEXP", "1"))
_KFIN = int(os.environ.get("KFIN", "1"))
_KORD = int(os.environ.get("KORD", "1"))
_KALIAS = int(os.environ.get("KALIAS", "1"))


class Op:
    __slots__ = ("idx", "eng", "fn", "dma", "deps", "signal", "ticket", "sem", "val", "mode")


class Sched:
    ENGS = ("pe", "dve", "act", "pool", "sp")

    def __init__(self, nc, n_dma_sems=32):
        self.nc = nc
        self.ops = []
        self.slots = {}
        self.n_dma_sems = n_dma_sems
        self.dma_count = [0] * n_dma_sems
        self.dma_rr = 0
        self.extra_reads = []

    def add(self, eng, fn, reads=(), writes=(), dma=False, mode=None):
        op = Op()
        op.idx = len(self.ops); op.eng = eng; op.fn = fn; op.dma = dma; op.mode = mode
        op.signal = False; op.ticket = None; op.sem = None; op.val = None
        deps = {}
        for k in list(reads) + self.extra_reads:
            s = self.slots.setdefault(k, [None, []])
            if s[0] is not None:
                deps[s[0].idx] = s[0]
            if not dma:
                s[1] = [r for r in s[1] if r.dma or r.eng != eng]
            s[1].append(op)
        for k in writes:
            s = self.slots.setdefault(k, [None, []])
            if s[0] is not None and s[0] is not op:
                deps[s[0].idx] = s[0]
            for r in s[1]:
                if r is not op:
                    deps[r.idx] = r
            s[0] = op; s[1] = []
        op.deps = list(deps.values())
        if dma:
            op.sem = self.dma_rr % self.n_dma_sems
            self.dma_rr += 1
            self.dma_count[op.sem] += 1
            op.val = 16 * self.dma_count[op.sem]
        self.ops.append(op)
        return op

    @staticmethod
    def _skip(p, c):
        return (not p.dma) and p.eng == "pe" and c.eng == "pe"

    def emit(self, stack):
        nc = self.nc
        for c in self.ops:
            for p in c.deps:
                if not self._skip(p, c) and not p.dma:
                    p.signal = True
        cnt = {e: 0 for e in self.ENGS}
        for o in self.ops:
            if o.signal:
                cnt[o.eng] += 1
                o.ticket = cnt[o.eng]
        esem = {e: stack.enter_context(nc.semaphore("s_" + e)) for e in self.ENGS}
        dsem = [stack.enter_context(nc.semaphore("d_%d" % i)) for i in range(self.n_dma_sems)]
        final_vals = [16 * c for c in self.dma_count]
        ops = self.ops
        block = stack.enter_context(nc.Block())

        def stream(ename, e):
            waited = {}

            def wait(key, sem, val):
                if waited.get(key, 0) >= val:
                    return
                e.wait_ge(sem, val)
                waited[key] = val

            cur_mode = None
            for o in ops:
                if o.eng != ename:
                    continue
                if o.mode is not None:
                    if _KDRAIN and cur_mode is not None and o.mode != cur_mode:
                        e.drain()
                    cur_mode = o.mode
                for p in o.deps:
                    if self._skip(p, o):
                        continue
                    if p.dma:
                        wait(("d", p.sem), dsem[p.sem], p.val)
                    else:
                        wait(("e", p.eng), esem[p.eng], p.ticket)
                if o.dma and o.val > 16:
                    wait(("d", o.sem), dsem[o.sem], o.val - 16)
                inst = o.fn(e)
                if o.dma:
                    inst.then_inc(dsem[o.sem], 16)
                elif o.signal:
                    inst.then_inc(esem[o.eng], 1)
            if ename == "sp":
                for i, v in enumerate(final_vals):
                    if v:
                        wait(("d", i), dsem[i], v)

        @block.tensor
        def _(e): stream("pe", e)

        @block.vector
        def _(e): stream("dve", e)

        @block.scalar
        def _(e): stream("act", e)

        @block.gpsimd
        def _(e): stream("pool", e)

        @block.sync
        def _(e): stream("sp", e)


class Carver:
    def __init__(self, big, nwords):
        self.big = big; self.nwords = nwords; self.off = 0

    def alloc(self, free_shape, dtype):
        n = int(np.prod(free_shape))
        words = n if dtype == F32 else (n + 1) // 2
        assert self.off + words <= self.nwords, ("SBUF carve overflow", self.off, words, self.nwords)
        v = self.big[:, self.off:self.off + words]
        self.off += words
        if dtype != F32:
            v = v.bitcast(dtype)
            if n != words * 2:
                v = v[:, 0:n]
        if len(free_shape) == 2:
            v = v.rearrange("p (a b) -> p a b", a=free_shape[0])
        elif len(free_shape) == 3:
            v = v.rearrange("p (a b c) -> p a b c", a=free_shape[0], b=free_shape[1])
        return v


def bcast_ap(view, col0, ncols, rep):
    pstride = view.ap[0][0]
    return bass.AP(view.tensor, view.offset + col0, [[pstride, 128], [1, ncols], [0, rep]])


def build_program():
    nc = bass.Bass("TRN2", target_bir_lowering=False)
    x = nc.dram_tensor("x", [SEQ, DM], F32, kind="ExternalInput").ap()
    w_in = nc.dram_tensor("w_in", [DM, IN_COLS], F32, kind="ExternalInput").ap()
    w_oc = nc.dram_tensor("w_oc", [DM, DM], F32, kind="ExternalInput").ap()
    w_oa = nc.dram_tensor("w_oa", [DM, DM], F32, kind="ExternalInput").ap()
    w_o = nc.dram_tensor("w_o", [DM, DM], F32, kind="ExternalInput").ap()
    vecs_d = nc.dram_tensor("vecs", [128, 48], F32, kind="ExternalInput").ap()
    fg_d = nc.dram_tensor("fg", [128, DM], F32, kind="ExternalInput").ap()
    gn_d = nc.dram_tensor("gn", [128, DM], F32, kind="ExternalInput").ap()
    idn_d = nc.dram_tensor("idn", [128, 128], F32, kind="ExternalInput").ap()
    etab_d = nc.dram_tensor("etab", [8, 128, 1536], BF16, kind="ExternalInput").ap()
    y = nc.dram_tensor("y", [SEQ, DM], F32, kind="ExternalOutput").ap()
    ha_scr = nc.dram_tensor("ha_scr", [DM, SEQ], BF16, kind="Internal").ap()

    w_in_v = w_in.rearrange("(c p) n -> p c n", p=128)
    ha_scr_v = ha_scr.rearrange("(c p) t -> p c t", p=128)

    with ExitStack() as st:
        NW = 212000 // 4
        big = st.enter_context(nc.sbuf_tensor("big", [128, NW], F32))
        banks = [st.enter_context(nc.psum_tensor("P%d" % i, [128, 512], F32)) for i in range(8)]
        S = Sched(nc)
        CV = Carver(big, NW)

        idf = CV.alloc((128,), F32)
        idb = CV.alloc((128,), BF16)
        ones2 = CV.alloc((64,), BF16)
        vecs = CV.alloc((48,), F32)
        hb = CV.alloc((16,), F32)
        stats = CV.alloc((3, 64), F32)
        stats2 = CV.alloc((3, 32), F32)
        carry = CV.alloc((8, 2), F32)
        fence_t = CV.alloc((8,), F32)
        gnt = CV.alloc((DM,), F32)
        base_off = CV.off

        def DMA(eng, out, in_, reads, writes):
            S.add(eng, lambda e: e.dma_start(out=out, in_=in_), reads=reads, writes=writes, dma=True)

        def _tile(n):
            return 32 if n <= 32 else (64 if n <= 64 else 128)

        def MM(out, lhsT, rhs, start, stop, reads, writes):
            mode = (_tile(lhsT.shape[0]), _tile(lhsT.shape[-1]))
            S.add("pe", lambda e: e.matmul(out, lhsT=lhsT, rhs=rhs, start=start, stop=stop),
                  reads=reads, writes=writes, mode=mode)

        def TR(out, in_, ident, reads, writes):
            mode = (_tile(in_.shape[0]), _tile(in_.shape[-1]))
            S.add("pe", lambda e: e.transpose(out=out, in_=in_, identity=ident), reads=reads, writes=writes, mode=mode)

        def ACT(out, in_, func, reads, writes, scale=None, bias=None, accum_out=None):
            kw = {}
            if scale is not None: kw["scale"] = scale
            if bias is not None: kw["bias"] = bias
            if accum_out is not None: kw["accum_out"] = accum_out
            S.add("act", lambda e: e.activation(out=out, in_=in_, func=func, **kw), reads=reads, writes=writes)

        def TT(eng, out, in0, in1, op, reads, writes):
            S.add(eng, lambda e: e.tensor_tensor(out=out, in0=in0, in1=in1, op=op), reads=reads, writes=writes)

        def TS(eng, out, in0, s1, s2, op0, op1, reads, writes):
            S.add(eng, lambda e: e.tensor_scalar(out=out, in0=in0, scalar1=s1, scalar2=s2, op0=op0, op1=op1),
                  reads=reads, writes=writes)

        def STT(out, in0, scalar, in1, op0, op1, reads, writes):
            S.add("dve", lambda e: e.scalar_tensor_tensor(out=out, in0=in0, scalar=scalar, in1=in1, op0=op0, op1=op1),
                  reads=reads, writes=writes)

        def COPY(eng, out, in_, reads, writes):
            S.add(eng, lambda e: e.tensor_copy(out, in_), reads=reads, writes=writes)

        def RECIP(out, in_, reads, writes):
            S.add("dve", lambda e: e.reciprocal(out=out, in_=in_), reads=reads, writes=writes)

        bank_rr = [0]

        def next_bank(pool=(0, 1, 2, 3, 4, 5, 6, 7)):
            b = pool[bank_rr[0] % len(pool)]
            bank_rr[0] += 1
            return b

        def bslot(b):
            return ("bank", b)

        DMA("sp", idf, idn_d[:, :], [], ["idf"])
        DMA("pool", idb, idn_d[:, :], [], ["idb"])
        DMA("sp", vecs, vecs_d[:, :], [], ["vecs"])
        DMA("sp", gnt, gn_d[:, :], [], ["gnt"])
        S.add("pool", lambda e: e.memset(ones2, 2.0), writes=["ones2"])
        S.add("pool", lambda e: e.memset(carry, 0.0), writes=["carry"])
        TS("pool", hb, vecs[:, 8:24], 0.5, 0.0, ALU.mult, ALU.add, ["vecs"], ["hb"])

        def norm_s1(t, col, xt_b, xt_slot, xb_b, xb_slot, junkb):
            DMA("sp", xt_b, x[t * 128:(t + 1) * 128, :], [], [xt_slot])
            ACT(junkb, xt_b, AF.Square, [xt_slot], ["junkb", ("ss", col)], accum_out=stats[:, 0, col:col + 1])
            ACT(stats[:, 1, col:col + 1], stats[:, 0, col:col + 1], AF.Sqrt, [("ss", col)], [("rs", col)],
                scale=1.0 / DM, bias=EPS)
            RECIP(stats[:, 2, col:col + 1], stats[:, 1, col:col + 1], [("rs", col)], [("rstd", col)])
            STT(xb_b, xt_b, stats[:, 2, col:col + 1], gnt, ALU.mult, ALU.mult,
                [xt_slot, ("rstd", col), "gnt"], [xb_slot])

        def norm_s2(xb_b, xb_slot, dst, dst_slot, bank_pool):
            for half in range(2):
                b = next_bank(bank_pool)
                pbf = banks[b][:, :].bitcast(BF16)
                for cc in range(4):
                    c = half * 4 + cc
                    TR(pbf[:, cc * 128:(cc + 1) * 128], xb_b[:, c * 128:(c + 1) * 128], idb,
                       [xb_slot, "idb"], [bslot(b)])
                src = pbf[:, 0:512].rearrange("p (c j) -> p c j", c=4)
                if half == 0:
                    ACT(dst[:, 0:4, :], src, AF.Copy, [bslot(b)], [dst_slot])
                else:
                    COPY("dve", dst[:, 4:8, :], src, [bslot(b)], [(dst_slot, 1)])

        def norm_steps(tiles, col0, xt_l, xb_l, junkb, dst_of, slot_of, bank_pool, ahead=1):
            nbuf = len(xt_l)

            def s1(i):
                t = tiles[i]
                norm_s1(t, col0 + t, xt_l[t % nbuf], ("xt", t % nbuf), xb_l[t % nbuf], ("xn", t % nbuf), junkb)

            def step(i):
                if i == 0:
                    for k in range(min(ahead, len(tiles))):
                        s1(k)
                if i + ahead < len(tiles):
                    s1(i + ahead)
                t = tiles[i]
                norm_s2(xb_l[t % nbuf], ("xn", t % nbuf), dst_of(i), slot_of(i), bank_pool)

            return [(lambda i=i: step(i)) for i in range(len(tiles))]

        S.extra_reads = ["R_AB"]
        CV.off = base_off
        uT = CV.alloc((NCH, SEQ), BF16)
        wB = CV.alloc((3, NCH, 128), BF16)
        wza = CV.alloc((2, NCH, 128), BF16)
        qT2 = [CV.alloc((SEQ,), BF16) for _ in range(2)]
        kT2 = [CV.alloc((SEQ,), BF16) for _ in range(2)]
        vT = CV.alloc((SEQ,), BF16)
        vP = [CV.alloc((32, 128), BF16) for _ in range(3)]
        NPT = 12
        NSS = 6
        PT = [CV.alloc((2, 256), BF16) for _ in range(NPT)]
        rawb = [CV.alloc((2, 256), BF16) for _ in range(NSS)]
        hat = [CV.alloc((512,), BF16) for _ in range(2)]
        Eb = CV.alloc((3, 2, 256), BF16)
        accN = CV.alloc((SEQ,), F32)
        accD = CV.alloc((SEQ,), F32)
        xt = [accN[:, 1024 * i:1024 * (i + 1)] for i in range(4)]
        xn = [accD[:, 512 * i:512 * (i + 1)].bitcast(BF16) for i in range(4)]
        junkb = accD[:, 2048:2560].bitcast(BF16)
        tzb = [CV.alloc((512,), F32) for _ in range(2)]
        stg = [CV.alloc((256,), F32) for _ in range(2)]
        if os.environ.get("KVERB"):
            print("phase AB carve words", CV.off, "of", NW)

        for st_ in norm_steps(list(range(32)), 0, xt, xn, junkb,
                              lambda i: uT[:, :, i * 128:(i + 1) * 128], lambda i: ("uT", i),
                              (0, 1, 2, 3, 4, 5, 6, 7), ahead=2):
            st_()

        ATT_BANKS = (0, 1, 6, 7)
        PV_BANKS = (2, 3)
        PRJ_BANKS = (4, 5)
        NHP = 8 if _STOP >= 4 else (1 if _STOP >= 2 else 0)
        NHP = min(NHP, int(os.environ.get("KNHP", "8")))

        def load_wB(hp):
            for j in range(3):
                col0 = 4096 + j * 1024 + hp * 128
                DMA("pool", wB[:, j, :, :], w_in_v[:, :, col0:col0 + 128], [], [("wB", j)])

        def load_wza(hp):
            col0 = 4096 + 3 * 1024 + hp * 128
            DMA("pool", wza[:, hp % 2, :, :], w_in_v[:, :, col0:col0 + 128], [], [("wza", hp % 2)])

        def load_Eb(hp):
            DMA("sp", Eb, etab_d[hp].rearrange("p (a b c) -> p a b c", a=3, b=2), [], ["Eb"])

        uT_all = [("uT", t) for t in range(32)]
        uT_all2 = [(("uT", t), 1) for t in range(32)]
        prj_rr = [0]

        def proj_groups(hp, interleaved):
            par = hp % 2
            groups = []
            for j, dstT, nm in ((0, qT2[par], ("qT", par)), (1, kT2[par], ("kT", par)), (2, vT, ("vT",))):
                for tt in range(8):
                    def g(j=j, dstT=dstT, nm=nm, tt=tt):
                        if interleaved:
                            b = PRJ_BANKS[prj_rr[0] % 2]; prj_rr[0] += 1
                        else:
                            b = next_bank(ATT_BANKS + PRJ_BANKS)
                        for c in range(NCH):
                            MM(banks[b][:, :], wB[:, j, c, :], uT[:, c, tt * 512:(tt + 1) * 512], c == 0, c == NCH - 1,
                               [("wB", j)] + uT_all[tt * 4:tt * 4 + 4] + uT_all2[tt * 4:tt * 4 + 4], [bslot(b)])
                        dst = dstT[:, tt * 512:(tt + 1) * 512]
                        slot = nm + (tt,)
                        if interleaved:
                            def ev(j=j, dst=dst, b=b, slot=slot):
                                if j == 0:
                                    TS("dve", dst, banks[b][:, :], 0.125, 0.0, ALU.mult, ALU.add,
                                       [bslot(b), "PVDONE"], [slot, "GEVAC"])
                                else:
                                    COPY("dve", dst, banks[b][:, :], [bslot(b), "PVDONE"], [slot, "GEVAC"])
                            return ev
                        else:
                            if j == 0:
                                ACT(dst, banks[b][:, :], AF.Copy, [bslot(b)], [slot], scale=0.125)
                            elif j == 1:
                                COPY("dve", dst, banks[b][:, :], [bslot(b)], [slot])
                            else:
                                ACT(dst, banks[b][:, :], AF.Copy, [bslot(b)], [slot])
                        return slot
                    groups.append(g)
            return groups

        def emit_vtrans(hp):
            for pi, d in enumerate(PATTERNS):
                nb = SEQ // (128 * d)
                for g in range(4):
                    b = next_bank(ATT_BANKS)
                    pbf = banks[b][:, :].bitcast(BF16)
                    for k in range(8):
                        blk = g * 8 + k
                        r, j = blk // nb, blk % nb
                        s0 = 128 * j * d + r
                        TR(pbf[:, k * 128:(k + 1) * 128], vT[:, s0:s0 + 127 * d + 1:d], idb,
                           [("vT", tt) for tt in range(8)] + ["idb"], [bslot(b)])
                    COPY("dve", vP[pi][:, g * 8:(g + 1) * 8, :], pbf.rearrange("p (b f) -> p b f", b=8),
                         [bslot(b)], [("vP", pi, g)])

        rr = {"ss": 0, "pt": 0, "stg": 0}

        def emit_S(u, par):
            pi, d, nb, r, jp = u
            qT = qT2[par]; kT = kT2[par]
            pts = []
            raws = []
            for a in range(2):
                b = next_bank(ATT_BANKS)
                for jj in range(2):
                    j = 2 * jp + jj
                    nq = 256 if j + 1 < nb else 128
                    s0 = 128 * j * d + r
                    lo_t = s0 // 512
                    rd_q = [("qT", par, tt) for tt in range(lo_t, min(7, (s0 + (nq - 1) * d) // 512) + 1)]
                    rd_k = [("kT", par, tt) for tt in range(lo_t, min(7, (s0 + 127 * d) // 512) + 1)]
                    MM(banks[b][:, jj * 256:jj * 256 + nq],
                       kT[64 * a:64 * a + 64, s0:s0 + 127 * d + 1:d],
                       qT[64 * a:64 * a + 64, s0:s0 + (nq - 1) * d + 1:d],
                       True, True, rd_q + rd_k, [bslot(b)])
                sb = rr["ss"] % NSS; rr["ss"] += 1
                pt = rr["pt"] % NPT; rr["pt"] += 1
                pts.append(pt)
                raws.append(("raw", sb))
                e_b = bass.AP(Eb.tensor, Eb.offset + (pi * 2 + a) * 256, [[Eb.ap[0][0], 128], [0, 2], [1, 256]])
                ACT(rawb[sb][:, :, :], banks[b][:, :].rearrange("p (k q) -> p k q", k=2), AF.Exp, [bslot(b)], [("raw", sb)])
                TT("pool", PT[pt][:, :, :], rawb[sb][:, :, :], e_b, ALU.mult,
                   [("raw", sb), "Eb"], [("PT", pt)])
            return pts + raws

        pvstate = {"n0": 0}

        def emit_PV(u, pts, prev, extra=()):
            pi, d, nb, r, jp = u
            extra = list(extra) + ["EVAC", "GEVAC"]
            for jj in range(2):
                j = 2 * jp + jj
                if j % 4 == 0:
                    pvstate["n0"] = j
                pvs = PV_BANKS; n0 = pvstate["n0"]
                col = (j % 4) * 128
                blk_c = r * nb + j
                for (pb, which) in ((pvs[0], "num"), (pvs[1], "den")):
                    for a in range(2):
                        outp = banks[pb][64 * a:64 * a + 64, col:col + 128]
                        pairs = []
                        if jj == 1 or prev is not None:
                            ptp = pts[a] if jj == 1 else prev[a]
                            kidx = 0 if jj == 1 else 1
                            lh = vP[pi][:, blk_c - 1, 64 * a:64 * a + 64] if which == "num" else ones2
                            pairs.append((lh, PT[ptp][:, kidx, 128:256], ("PT", ptp), ("vP", pi, (blk_c - 1) // 8)))
                        lh = vP[pi][:, blk_c, 64 * a:64 * a + 64] if which == "num" else ones2
                        pairs.append((lh, PT[pts[a]][:, jj, 0:128], ("PT", pts[a]), ("vP", pi, blk_c // 8)))
                        for ii, (lh_, rh_, s1_, s2_) in enumerate(pairs):
                            MM(outp, lh_, rh_, ii == 0, ii == len(pairs) - 1, [s1_, s2_, "ones2"] + extra,
                               [bslot(pb), "PVDONE"])
                if j % 4 == 3 or j == nb - 1:
                    cnt = j - n0 + 1
                    t0 = 128 * n0 * d + r
                    nel = cnt * 128
                    lo, hi = t0 // 512, (t0 + (nel - 1) * d) // 512
                    for (pb, acc, nm) in ((pvs[0], accN, "accN"), (pvs[1], accD, "accD")):
                        dst = acc[:, t0:t0 + (nel - 1) * d + 1:d]
                        sl = [(nm, tt) for tt in range(lo, hi + 1)]
                        if pi == 0:
                            COPY("dve", dst, banks[pb][:, 0:nel], [bslot(pb)], sl + ["EVAC"])
                        else:
                            TT("dve", dst, banks[pb][:, 0:nel], dst, ALU.add, [bslot(pb)] + sl, sl + ["EVAC"])

        def emit_attention(hp, groups):
            par = hp % 2
            units = []
            for pi, d in enumerate(PATTERNS[:_NPAT]):
                nb = SEQ // (128 * d)
                for r in range(d):
                    for jp in range(nb // 2):
                        units.append((pi, d, nb, r, jp))
            blocks = [list(range(i, min(i + _KBLK, len(units)))) for i in range(0, len(units), _KBLK)]
            groups = list(groups)
            ptsl = {}
            for ui in blocks[0]:
                ptsl[ui] = emit_S(units[ui], par)
            for bi, blk in enumerate(blocks):
                extra = []
                if bi + 1 < len(blocks):
                    for ui in blocks[bi + 1]:
                        ptsl[ui] = emit_S(units[ui], par)
                        extra += ptsl[ui][2:4]
                ev = groups.pop(0)() if groups else None
                fin_tail = None
                u0 = units[blk[0]]
                if _KFIN and u0[1] == 1 and _KBLK == 2:
                    fin_tail = finalize_tile(hp, u0[4] // 2)
                for ui in blk:
                    u = units[ui]
                    emit_PV(u, ptsl[ui][0:2], ptsl[ui - 1][0:2] if u[4] > 0 else None, extra)
                if ev is not None:
                    ev()
                if fin_tail is not None:
                    fin_tail()
            for g in groups:
                g()()

        def finalize_tile(hp, tt):
            b = PRJ_BANKS[prj_rr[0] % 2]; prj_rr[0] += 1
            for c in range(NCH):
                MM(banks[b][:, :], wza[:, hp % 2, c, :], uT[:, c, tt * 512:(tt + 1) * 512], c == 0, c == NCH - 1,
                   [("wza", hp % 2)] + uT_all[tt * 4:tt * 4 + 4] + uT_all2[tt * 4:tt * 4 + 4], [bslot(b)])

            def tail():
                k2 = tt % 2
                sl = [("accD", tt)]
                ACT(tzb[k2], banks[b][:, :], AF.Tanh, [bslot(b), "PVDONE"], [("tzb", k2), "GEVAC"], scale=0.5)
                STT(tzb[k2], tzb[k2], 1.0, banks[b][:, :], ALU.add, ALU.mult, [("tzb", k2), bslot(b), "PVDONE"],
                    [("tzb", k2), "GEVAC"])
                RECIP(accD[:, tt * 512:(tt + 1) * 512], accD[:, tt * 512:(tt + 1) * 512], sl, sl)
                TT("pool", accN[:, tt * 512:(tt + 1) * 512], accN[:, tt * 512:(tt + 1) * 512],
                   accD[:, tt * 512:(tt + 1) * 512], ALU.mult, [("accD", tt), ("accN", tt)], [("accN", tt)])
                TT("pool", hat[k2], accN[:, tt * 512:(tt + 1) * 512], tzb[k2], ALU.mult,
                   [("accN", tt), ("tzb", k2)], [("hat", k2)])
                DMA("sp", ha_scr[hp * 128:(hp + 1) * 128, tt * 512:(tt + 1) * 512], hat[k2],
                    [("hat", k2)], [("ha_scr", tt)])
            return tail

        def emit_finalize(hp):
            for tt in range(8):
                sl = [("accD", tt)]
                RECIP(accD[:, tt * 512:(tt + 1) * 512], accD[:, tt * 512:(tt + 1) * 512], sl, sl)
            for tt in range(8):
                b = next_bank(ATT_BANKS)
                for c in range(NCH):
                    MM(banks[b][:, :], wza[:, hp % 2, c, :], uT[:, c, tt * 512:(tt + 1) * 512], c == 0, c == NCH - 1,
                       [("wza", hp % 2)] + uT_all[tt * 4:tt * 4 + 4] + uT_all2[tt * 4:tt * 4 + 4], [bslot(b)])
                k2 = tt % 2
                ACT(tzb[k2], banks[b][:, :], AF.Tanh, [bslot(b)], [("tzb", k2)], scale=0.5)
                STT(tzb[k2], tzb[k2], 1.0, banks[b][:, :], ALU.add, ALU.mult, [("tzb", k2), bslot(b)], [("tzb", k2)])
                TT("pool", accN[:, tt * 512:(tt + 1) * 512], accN[:, tt * 512:(tt + 1) * 512],
                   accD[:, tt * 512:(tt + 1) * 512], ALU.mult, [("accD", tt), ("accN", tt)], [("accN", tt)])
                TT("pool", hat[k2], accN[:, tt * 512:(tt + 1) * 512], tzb[k2], ALU.mult,
                   [("accN", tt), ("tzb", k2)], [("hat", k2)])
                DMA("sp", ha_scr[hp * 128:(hp + 1) * 128, tt * 512:(tt + 1) * 512], hat[k2],
                    [("hat", k2)], [("ha_scr", tt)])

        if NHP:
            load_wB(0)
            load_wza(0)
            load_Eb(0)
            for g in proj_groups(0, False):
                g()
        for hp in range(NHP):
            if hp + 1 < NHP:
                load_wB(hp + 1)
                load_wza(hp + 1)
            emit_vtrans(hp)
            emit_attention(hp, proj_groups(hp + 1, True) if hp + 1 < NHP else [])
            if hp + 1 < NHP:
                load_Eb(hp + 1)
            if not (_KFIN and _KBLK == 2 and PATTERNS[-1] == 1 and _NPAT == 3):
                emit_finalize(hp)

        S.extra_reads = []
        S.add("dve", lambda e: e.memset(fence_t, 0.0), writes=["R_AB", "FENCE"])
        S.extra_reads = ["FENCE"]
        CV.off = base_off
        wOC = CV.alloc((NCH, DM), BF16)
        wOA = CV.alloc((NCH, DM), BF16)
        wO = CV.alloc((NCH, DM), BF16)
        wGC = CV.alloc((NCH, DM), BF16)
        wGA = CV.alloc((NCH, DM), BF16)
        wC = [CV.alloc((4, NCH, 128), BF16) for _ in range(2)]
        uTq = CV.alloc((NCH, 1024), BF16)
        hcq = CV.alloc((NCH, 1024), BF16)
        haq = CV.alloc((NCH, 512), BF16)
        mrg = CV.alloc((NCH, 512), BF16)
        junkb = CV.alloc((DM,), BF16)
        fgt = CV.alloc((DM,), F32)
        xt = [CV.alloc((DM,), F32) for _ in range(2)]
        xn = [CV.alloc((DM,), BF16) for _ in range(2)]
        hsb = [CV.alloc((DM,), F32) for _ in range(2)]
        u1 = [CV.alloc((514,), F32) for _ in range(2)]
        cgs = [CV.alloc((512,), F32) for _ in range(2)]
        cvb = [CV.alloc((512,), F32) for _ in range(2)]
        tzc = [CV.alloc((512,), F32) for _ in range(2)]
        g1b = [CV.alloc((512,), F32) for _ in range(2)]
        g2b = [CV.alloc((512,), F32) for _ in range(2)]
        m1b = [CV.alloc((512,), F32)] * 2
        m2b = [CV.alloc((512,), F32)] * 2
        if os.environ.get("KVERB"):
            print("phase C carve words", CV.off, "of", NW)

        DMA("sp", fgt, fg_d[:, :], [], ["fgt"])

        def load_wC(qi, cc):
            sl = (qi * 8 + cc) % 2
            for j in range(4):
                col0 = j * 1024 + cc * 128
                DMA("pool", wC[sl][:, j, :, :], w_in_v[:, :, col0:col0 + 128], [], [("wC", sl, j)])

        resident_q = []
        for (dst, src, nm) in ((wOC, w_oc, "wOC"), (wGC, None, "wGC"), (wOA, w_oa, "wOA"), (wGA, None, "wGA"), (wO, w_o, "wO")):
            for c in range(NCH):
                resident_q.append((dst, src, nm, c))

        def load_resident(n):
            for _ in range(n):
                if not resident_q:
                    return
                dst, src, nm, c = resident_q.pop(0)
                if src is None:
                    col0 = 8192 if nm == "wGC" else 9216
                    DMA("pool", dst[:, c, :], w_in[c * 128:(c + 1) * 128, col0:col0 + 1024], [], [(nm, c)])
                else:
                    DMA("pool", dst[:, c, :], src[c * 128:(c + 1) * 128, :], [], [(nm, c)])

        load_wC(0, 0)
        ALLB = (0, 1, 2, 3, 4, 5, 6, 7)

        def c1_steps(q_):
            return norm_steps([q_ * 8 + k for k in range(8)], 32, xt, xn, junkb,
                              lambda i: uTq[:, :, i * 128:(i + 1) * 128], lambda i: ("uTq", i), ALLB, ahead=1)

        def load_haq(T0_):
            DMA("sp", haq, ha_scr_v[:, :, T0_:T0_ + 512], [("ha_scr", T0_ // 512)], ["haq"])
        step = 0
        fin_box = [0]
        for qi in range(4 if _STOP >= 5 else 0):
            if qi == 0:
                for st_ in c1_steps(0):
                    st_()
            for cc in range(NCH):
                sl = (qi * 8 + cc) % 2
                if cc + 1 < NCH:
                    load_wC(qi, cc + 1)
                elif qi + 1 < 4:
                    load_wC(qi + 1, 0)
                for hf in range(2):
                    load_resident(3)
                    k2 = step % 2; step += 1
                    tsl = [("uTq", hf * 4 + i) for i in range(4)] + [(("uTq", hf * 4 + i), 1) for i in range(4)]
                    bk = {}
                    for j in (0, 2, 3, 1):
                        b = next_bank(ALLB); bk[j] = b
                        for c in range(NCH):
                            MM(banks[b][:, :], wC[sl][:, j, c, :], uTq[:, c, hf * 512:(hf + 1) * 512], c == 0, c == NCH - 1,
                               [("wC", sl, j)] + tsl, [bslot(b)])
                    ACT(cgs[k2], banks[bk[2]][:, :], AF.Copy, [bslot(bk[2])], [("cgs", k2)])
                    COPY("pool", u1[k2][:, 0:2], carry[:, cc, :], ["carry"], [("u1", k2)])
                    TT("dve", u1[k2][:, 2:514], banks[bk[0]][:, :], cgs[k2], ALU.mult,
                       [bslot(bk[0]), ("cgs", k2), ("u1", k2)], [("u1", k2)])
                    COPY("pool", carry[:, cc, :], u1[k2][:, 512:514], [("u1", k2)], ["carry"])
                    ACT(tzc[k2], banks[bk[3]][:, :], AF.Tanh, [bslot(bk[3])], [("tzc", k2)], scale=0.5)
                    STT(tzc[k2], tzc[k2], 1.0, banks[bk[3]][:, :], ALU.add, ALU.mult, [("tzc", k2), bslot(bk[3])], [("tzc", k2)])
                    STT(tzc[k2], banks[bk[1]][:, :], 0.5, tzc[k2], ALU.mult, ALU.mult,
                        [bslot(bk[1]), ("tzc", k2)], [("tzc", k2)])
                    w0 = vecs[:, 24 + cc:25 + cc]; w1 = vecs[:, 32 + cc:33 + cc]; w2 = vecs[:, 40 + cc:41 + cc]
                    TS("pool", cvb[k2], u1[k2][:, 0:512], w0, 0.0, ALU.mult, ALU.add, [("u1", k2), "vecs"], [("cvb", k2)])
                    STT(cvb[k2], u1[k2][:, 1:513], w1, cvb[k2], ALU.mult, ALU.add, [("u1", k2), ("cvb", k2), "vecs"], [("cvb", k2)])
                    STT(cvb[k2], u1[k2][:, 2:514], w2, cvb[k2], ALU.mult, ALU.add, [("u1", k2), ("cvb", k2), "vecs"], [("cvb", k2)])
                    TT("pool", hcq[:, cc, hf * 512:(hf + 1) * 512], tzc[k2], cvb[k2], ALU.mult,
                       [("tzc", k2), ("cvb", k2)], [("hcq", cc, hf)])
            load_resident(100)
            for hf in range(2):
                T0 = qi * 1024 + hf * 512
                if qi == 0 and hf == 0:
                    load_haq(T0)
                tsl = [("uTq", hf * 4 + i) for i in range(4)] + [(("uTq", hf * 4 + i), 1) for i in range(4)]
                hsl = [("hcq", c, hf) for c in range(NCH)]
                for dc in range(NCH):
                    k2 = dc % 2
                    bk = [next_bank(ALLB) for _ in range(4)]
                    specs = ((wOC, "wOC", hcq[:, :, hf * 512:(hf + 1) * 512], hsl),
                             (wGC, "wGC", uTq[:, :, hf * 512:(hf + 1) * 512], tsl),
                             (wOA, "wOA", haq, ["haq"]),
                             (wGA, "wGA", uTq[:, :, hf * 512:(hf + 1) * 512], tsl))
                    for i, (wt, nm, rhs3, rsl) in enumerate(specs):
                        for c in range(NCH):
                            MM(banks[bk[i]][:, :], wt[:, c, dc * 128:(dc + 1) * 128], rhs3[:, c, :], c == 0, c == NCH - 1,
                               [(nm, c)] + rsl, [bslot(bk[i])])
                    ACT(g1b[k2], banks[bk[1]][:, :], AF.Tanh, [bslot(bk[1]), "hb"], [("g1b", k2)], scale=0.5, bias=hb[:, dc:dc + 1])
                    ACT(g2b[k2], banks[bk[3]][:, :], AF.Tanh, [bslot(bk[3]), "hb"], [("g2b", k2)], scale=0.5, bias=hb[:, 8 + dc:9 + dc])
                    TS("pool", g1b[k2], g1b[k2], 0.5, 0.5, ALU.mult, ALU.add, [("g1b", k2)], [("g1b", k2)])
                    TS("pool", g2b[k2], g2b[k2], 0.5, 0.5, ALU.mult, ALU.add, [("g2b", k2)], [("g2b", k2)])
                    TT("dve", m1b[0], banks[bk[0]][:, :], g1b[k2], ALU.mult, [bslot(bk[0]), ("g1b", k2)], [("m1b", 0)])
                    TT("dve", m2b[0], banks[bk[2]][:, :], g2b[k2], ALU.mult, [bslot(bk[2]), ("g2b", k2)], [("m2b", 0)])
                    TT("pool", mrg[:, dc, :], m1b[0], m2b[0], ALU.add, [("m1b", 0), ("m2b", 0)], [("mrg", dc)])
                msl = [("mrg", dc) for dc in range(NCH)]
                if hf == 0:
                    load_haq(T0 + 512)
                elif qi + 1 < 4:
                    load_haq(T0 + 512)

                def out_tk(tk, T0=T0, msl=msl):
                    nonlocal_fin = fin_box
                    t = T0 // 128 + tk
                    k2 = nonlocal_fin[0] % 2; nonlocal_fin[0] += 1
                    hs = [("hsb", k2, 0), ("hsb", k2, 1)]
                    DMA("sp", hsb[k2], x[t * 128:(t + 1) * 128, :], [], hs)
                    for nh in range(2):
                        b = next_bank(ALLB)
                        for c in range(NCH):
                            MM(banks[b][:, :], mrg[:, c, tk * 128:(tk + 1) * 128], wO[:, c, nh * 512:(nh + 1) * 512],
                               c == 0, c == NCH - 1, msl + [("wO", c)], [bslot(b)])
                        TT("dve", hsb[k2][:, nh * 512:(nh + 1) * 512], banks[b][:, :], hsb[k2][:, nh * 512:(nh + 1) * 512],
                           ALU.add, [bslot(b), ("hsb", k2, nh)], [("hsb", k2, nh)])
                    ACT(junkb, hsb[k2], AF.Square, hs, ["junkb", ("ss2", t)], accum_out=stats2[:, 0, t:t + 1])
                    ACT(stats2[:, 1, t:t + 1], stats2[:, 0, t:t + 1], AF.Sqrt, [("ss2", t)], [("rs2", t)], scale=1.0 / DM, bias=EPS)
                    RECIP(stats2[:, 2, t:t + 1], stats2[:, 1, t:t + 1], [("rs2", t)], [("rstd2", t)])
                    STT(hsb[k2], hsb[k2], stats2[:, 2, t:t + 1], fgt, ALU.mult, ALU.mult, hs + [("rstd2", t), "fgt"], hs)
                    DMA("sp", y[t * 128:(t + 1) * 128, :], hsb[k2], hs, [("y", t)])

                if hf == 1 and qi + 1 < 4:
                    nxt = c1_steps(qi + 1)
                    for i in range(8):
                        nxt[i]()
                        if i % 2 == 1:
                            out_tk(i // 2)
                else:
                    for tk in range(4):
                        out_tk(tk)

        S.emit(st)
    return nc


def _consts():
    slopes = np.exp2(-8.0 * np.arange(1, 17, dtype=np.float32) / np.float32(16)).astype(np.float32)
    ik = np.arange(128)[:, None]
    iq = np.arange(128)[None, :]
    tab = np.empty((8, 128, 3, 2, 256), np.float32)
    for h in range(16):
        for pi, d in enumerate(PATTERNS):
            cur_delta = (iq - ik)
            nxt_delta = (128 + iq - ik)
            cur = np.where(cur_delta >= 0, -slopes[h] * (d * cur_delta).astype(np.float32), np.float32(NEG))
            nxt = np.where(nxt_delta <= 128, -slopes[h] * (d * nxt_delta).astype(np.float32), np.float32(NEG))
            tab[h // 2, :, pi, h % 2, 0:128] = cur
            tab[h // 2, :, pi, h % 2, 128:256] = nxt
    etab = np.exp(tab.astype(np.float64)).astype(np.float32).astype(ml_dtypes.bfloat16)
    return etab.reshape(8, 128, 1536), np.eye(128, dtype=np.float32)


def kernel(x, norm_g, w_in, b_merge, conv_w, w_out_conv, w_out_attn, w_o, final_g):
    x = np.asarray(x, np.float32)
    n = 8
    etab, idn = _consts()
    vecs = np.concatenate([
        np.asarray(norm_g, np.float32).reshape(8, 128).T,
        np.asarray(b_merge, np.float32).reshape(16, 128).T,
        np.asarray(conv_w, np.float32).reshape(3, 8, 128).transpose(2, 0, 1).reshape(128, 24),
    ], axis=1)
    vecs = np.ascontiguousarray(vecs, dtype=np.float32)
    fg = np.ascontiguousarray(np.broadcast_to(np.asarray(final_g, np.float32)[None, :], (128, DM)))
    gn = np.ascontiguousarray(np.broadcast_to(np.asarray(norm_g, np.float32).reshape(1, DM), (128, DM)))
    shared = {
        "w_in": np.ascontiguousarray(np.asarray(w_in, np.float32)[0]),
        "w_oc": np.ascontiguousarray(np.asarray(w_out_conv, np.float32)[0]),
        "w_oa": np.ascontiguousarray(np.asarray(w_out_attn, np.float32)[0]),
        "w_o": np.ascontiguousarray(np.asarray(w_o, np.float32)[0]),
        "vecs": vecs, "fg": fg, "gn": gn, "idn": idn, "etab": etab,
    }
    nc = build_program()
    in_maps = [dict(shared, x=np.ascontiguousarray(x[i])) for i in range(n)]
    res = run_bass_kernel_spmd(nc, in_maps, core_ids=list(range(n)))
    return np.stack([np.asarray(r["y"], dtype=np.float32) for r in res.results], axis=0)
```

```python
import os
import numpy as np
import ml_dtypes
import concourse.bass as bass
import concourse.mybir as mybir
from concourse.bass_utils import run_bass_kernel_spmd
from contextlib import ExitStack

F32 = mybir.dt.float32
BF16 = mybir.dt.bfloat16
AF = mybir.ActivationFunctionType
ALU = mybir.AluOpType

SEQ = 4096
DM = 1024
NCH = 8
IN_COLS = 10240
EPS = 1e-6
PATTERNS = (16, 4, 1)
NEG = -30000.0
_STOP = int(os.environ.get("KSTOP", "9"))
_SUB = int(os.environ.get("KSUB", "9"))
_NPAT = int(os.environ.get("KPAT", "3"))
_NOEW = int(os.environ.get("KNOEW", "0"))
_KLN = int(os.environ.get("KLN", "0"))
_KP2 = int(os.environ.get("KP2", "1"))
_KPIPE = int(os.environ.get("KPIPE", "3"))
_KDRAIN = int(os.environ.get("KDRAIN", "0"))
_KBLK = int(os.environ.get("KBLK", "2"))
_KEXP = int(os.environ.get("KEXP", "1"))
_KFIN = int(os.environ.get("KFIN", "1"))
_KORD = int(os.environ.get("KORD", "1"))
_KALIAS = int(os.environ.get("KALIAS", "1"))


class Op:
    __slots__ = ("idx", "eng", "fn", "dma", "deps", "signal", "ticket", "sem", "val", "mode")


class Sched:
    ENGS = ("pe", "dve", "act", "pool", "sp")

    def __init__(self, nc, n_dma_sems=32):
        self.nc = nc
        self.ops = []
        self.slots = {}
        self.n_dma_sems = n_dma_sems
        self.dma_count = [0] * n_dma_sems
        self.dma_rr = 0
        self.dma_rr_sw = 0
        self.extra_reads = []

    def add(self, eng, fn, reads=(), writes=(), dma=False, mode=None):
        op = Op()
        op.idx = len(self.ops); op.eng = eng; op.fn = fn; op.dma = dma; op.mode = mode
        op.signal = False; op.ticket = None; op.sem = None; op.val = None
        deps = {}
        for k in list(reads) + self.extra_reads:
            s = self.slots.setdefault(k, [None, []])
            if s[0] is not None:
                deps[s[0].idx] = s[0]
            if not dma:
                s[1] = [r for r in s[1] if r.dma or r.eng != eng]
            s[1].append(op)
        for k in writes:
            s = self.slots.setdefault(k, [None, []])
            if s[0] is not None and s[0] is not op:
                deps[s[0].idx] = s[0]
            for r in s[1]:
                if r is not op:
                    deps[r.idx] = r
            s[0] = op; s[1] = []
        op.deps = list(deps.values())
        if dma:
            n_sw = self.n_dma_sems // 3
            if eng == "pool":
                op.sem = self.dma_rr_sw % n_sw
                self.dma_rr_sw += 1
            else:
                op.sem = n_sw + self.dma_rr % (self.n_dma_sems - n_sw)
                self.dma_rr += 1
            self.dma_count[op.sem] += 1
            op.val = 16 * self.dma_count[op.sem]
        self.ops.append(op)
        return op

    @staticmethod
    def _skip(p, c):
        return (not p.dma) and p.eng == "pe" and c.eng == "pe"

    def emit(self, stack):
        nc = self.nc
        for c in self.ops:
            for p in c.deps:
                if not self._skip(p, c) and not p.dma:
                    p.signal = True
        cnt = {e: 0 for e in self.ENGS}
        for o in self.ops:
            if o.signal:
                cnt[o.eng] += 1
                o.ticket = cnt[o.eng]
        esem = {e: stack.enter_context(nc.semaphore("s_" + e)) for e in self.ENGS}
        dsem = [stack.enter_context(nc.semaphore("d_%d" % i)) for i in range(self.n_dma_sems)]
        final_vals = [16 * c for c in self.dma_count]
        ops = self.ops
        block = stack.enter_context(nc.Block())

        def stream(ename, e):
            waited = {}

            def wait(key, sem, val):
                if waited.get(key, 0) >= val:
                    return
                e.wait_ge(sem, val)
                waited[key] = val

            cur_mode = None
            for o in ops:
                if o.eng != ename:
                    continue
                if o.mode is not None:
                    if _KDRAIN and cur_mode is not None and o.mode != cur_mode:
                        e.drain()
                    cur_mode = o.mode
                for p in o.deps:
                    if self._skip(p, o):
                        continue
                    if p.dma:
                        wait(("d", p.sem), dsem[p.sem], p.val)
                    else:
                        wait(("e", p.eng), esem[p.eng], p.ticket)
                if o.dma and o.val > 16:
                    wait(("d", o.sem), dsem[o.sem], o.val - 16)
                inst = o.fn(e)
                if o.dma:
                    inst.then_inc(dsem[o.sem], 16)
                elif o.signal:
                    inst.then_inc(esem[o.eng], 1)
            if ename == "sp":
                for i, v in enumerate(final_vals):
                    if v:
                        wait(("d", i), dsem[i], v)

        @block.tensor
        def _(e): stream("pe", e)

        @block.vector
        def _(e): stream("dve", e)

        @block.scalar
        def _(e): stream("act", e)

        @block.gpsimd
        def _(e): stream("pool", e)

        @block.sync
        def _(e): stream("sp", e)


class Carver:
    def __init__(self, big, nwords):
        self.big = big; self.nwords = nwords; self.off = 0

    def alloc(self, free_shape, dtype):
        n = int(np.prod(free_shape))
        words = n if dtype == F32 else (n + 1) // 2
        assert self.off + words <= self.nwords, ("SBUF carve overflow", self.off, words, self.nwords)
        v = self.big[:, self.off:self.off + words]
        self.off += words
        if dtype != F32:
            v = v.bitcast(dtype)
            if n != words * 2:
                v = v[:, 0:n]
        if len(free_shape) == 2:
            v = v.rearrange("p (a b) -> p a b", a=free_shape[0])
        elif len(free_shape) == 3:
            v = v.rearrange("p (a b c) -> p a b c", a=free_shape[0], b=free_shape[1])
        return v


def bcast_ap(view, col0, ncols, rep):
    pstride = view.ap[0][0]
    return bass.AP(view.tensor, view.offset + col0, [[pstride, 128], [1, ncols], [0, rep]])


def build_program():
    nc = bass.Bass("TRN2", target_bir_lowering=False)
    x = nc.dram_tensor("x", [SEQ, DM], F32, kind="ExternalInput").ap()
    w_in = nc.dram_tensor("w_in", [DM, IN_COLS], F32, kind="ExternalInput").ap()
    w_oc = nc.dram_tensor("w_oc", [DM, DM], F32, kind="ExternalInput").ap()
    w_oa = nc.dram_tensor("w_oa", [DM, DM], F32, kind="ExternalInput").ap()
    w_o = nc.dram_tensor("w_o", [DM, DM], F32, kind="ExternalInput").ap()
    vecs_d = nc.dram_tensor("vecs", [128, 48], F32, kind="ExternalInput").ap()
    fg_d = nc.dram_tensor("fg", [128, DM], F32, kind="ExternalInput").ap()
    gn_d = nc.dram_tensor("gn", [128, DM], F32, kind="ExternalInput").ap()
    idn_d = nc.dram_tensor("idn", [128, 128], F32, kind="ExternalInput").ap()
    etab_d = nc.dram_tensor("etab", [8, 128, 1536], BF16, kind="ExternalInput").ap()
    y = nc.dram_tensor("y", [SEQ, DM], F32, kind="ExternalOutput").ap()
    ha_scr = nc.dram_tensor("ha_scr", [DM, SEQ], BF16, kind="Internal").ap()

    w_in_v = w_in.rearrange("(c p) n -> p c n", p=128)
    ha_scr_v = ha_scr.rearrange("(c p) t -> p c t", p=128)

    with ExitStack() as st:
        NW = 212000 // 4
        big = st.enter_context(nc.sbuf_tensor("big", [128, NW], F32))
        banks = [st.enter_context(nc.psum_tensor("P%d" % i, [128, 512], F32)) for i in range(8)]
        S = Sched(nc)
        CV = Carver(big, NW)

        idf = CV.alloc((128,), F32)
        idb = CV.alloc((128,), BF16)
        ones2 = CV.alloc((64,), BF16)
        vecs = CV.alloc((48,), F32)
        hb = CV.alloc((16,), F32)
        stats = CV.alloc((3, 64), F32)
        stats2 = CV.alloc((3, 32), F32)
        carry = CV.alloc((8, 2), F32)
        fence_t = CV.alloc((8,), F32)
        gnt = CV.alloc((DM,), F32)
        base_off = CV.off

        def DMA(eng, out, in_, reads, writes):
            S.add(eng, lambda e: e.dma_start(out=out, in_=in_), reads=reads, writes=writes, dma=True)

        def _tile(n):
            return 32 if n <= 32 else (64 if n <= 64 else 128)

        def MM(out, lhsT, rhs, start, stop, reads, writes):
            mode = (_tile(lhsT.shape[0]), _tile(lhsT.shape[-1]))
            S.add("pe", lambda e: e.matmul(out, lhsT=lhsT, rhs=rhs, start=start, stop=stop),
                  reads=reads, writes=writes, mode=mode)

        def TR(out, in_, ident, reads, writes):
            mode = (_tile(in_.shape[0]), _tile(in_.shape[-1]))
            S.add("pe", lambda e: e.transpose(out=out, in_=in_, identity=ident), reads=reads, writes=writes, mode=mode)

        def ACT(out, in_, func, reads, writes, scale=None, bias=None, accum_out=None):
            kw = {}
            if scale is not None: kw["scale"] = scale
            if bias is not None: kw["bias"] = bias
            if accum_out is not None: kw["accum_out"] = accum_out
            S.add("act", lambda e: e.activation(out=out, in_=in_, func=func, **kw), reads=reads, writes=writes)

        def TT(eng, out, in0, in1, op, reads, writes):
            S.add(eng, lambda e: e.tensor_tensor(out=out, in0=in0, in1=in1, op=op), reads=reads, writes=writes)

        def TS(eng, out, in0, s1, s2, op0, op1, reads, writes):
            S.add(eng, lambda e: e.tensor_scalar(out=out, in0=in0, scalar1=s1, scalar2=s2, op0=op0, op1=op1),
                  reads=reads, writes=writes)

        def STT(out, in0, scalar, in1, op0, op1, reads, writes):
            S.add("dve", lambda e: e.scalar_tensor_tensor(out=out, in0=in0, scalar=scalar, in1=in1, op0=op0, op1=op1),
                  reads=reads, writes=writes)

        def COPY(eng, out, in_, reads, writes):
            S.add(eng, lambda e: e.tensor_copy(out, in_), reads=reads, writes=writes)

        def RECIP(out, in_, reads, writes):
            S.add("dve", lambda e: e.reciprocal(out=out, in_=in_), reads=reads, writes=writes)

        bank_rr = [0]

        def next_bank(pool=(0, 1, 2, 3, 4, 5, 6, 7)):
            b = pool[bank_rr[0] % len(pool)]
            bank_rr[0] += 1
            return b

        def bslot(b):
            return ("bank", b)

        DMA("sp", idf, idn_d[:, :], [], ["idf"])
        DMA("pool", idb, idn_d[:, :], [], ["idb"])
        DMA("sp", vecs, vecs_d[:, :], [], ["vecs"])
        DMA("sp", gnt, gn_d[:, :], [], ["gnt"])
        S.add("pool", lambda e: e.memset(ones2, 2.0), writes=["ones2"])
        S.add("pool", lambda e: e.memset(carry, 0.0), writes=["carry"])
        TS("pool", hb, vecs[:, 8:24], 0.5, 0.0, ALU.mult, ALU.add, ["vecs"], ["hb"])

        def norm_s1(t, col, xt_b, xt_slot, xb_b, xb_slot, junkb):
            DMA("sp", xt_b, x[t * 128:(t + 1) * 128, :], [], [xt_slot])
            ACT(junkb, xt_b, AF.Square, [xt_slot], ["junkb", ("ss", col)], accum_out=stats[:, 0, col:col + 1])
            ACT(stats[:, 1, col:col + 1], stats[:, 0, col:col + 1], AF.Sqrt, [("ss", col)], [("rs", col)],
                scale=1.0 / DM, bias=EPS)
            RECIP(stats[:, 2, col:col + 1], stats[:, 1, col:col + 1], [("rs", col)], [("rstd", col)])
            STT(xb_b, xt_b, stats[:, 2, col:col + 1], gnt, ALU.mult, ALU.mult,
                [xt_slot, ("rstd", col), "gnt"], [xb_slot])

        def norm_s2(xb_b, xb_slot, dst, dst_slot, bank_pool):
            for half in range(2):
                b = next_bank(bank_pool)
                pbf = banks[b][:, :].bitcast(BF16)
                for cc in range(4):
                    c = half * 4 + cc
                    TR(pbf[:, cc * 128:(cc + 1) * 128], xb_b[:, c * 128:(c + 1) * 128], idb,
                       [xb_slot, "idb"], [bslot(b)])
                src = pbf[:, 0:512].rearrange("p (c j) -> p c j", c=4)
                if half == 0:
                    ACT(dst[:, 0:4, :], src, AF.Copy, [bslot(b)], [dst_slot])
                else:
                    COPY("dve", dst[:, 4:8, :], src, [bslot(b)], [(dst_slot, 1)])

        def norm_steps(tiles, col0, xt_l, xb_l, junkb, dst_of, slot_of, bank_pool, ahead=1):
            nbuf = len(xt_l)

            def s1(i):
                t = tiles[i]
                norm_s1(t, col0 + t, xt_l[t % nbuf], ("xt", t % nbuf), xb_l[t % nbuf], ("xn", t % nbuf), junkb)

            def step(i):
                if i == 0:
                    for k in range(min(ahead, len(tiles))):
                        s1(k)
                if i + ahead < len(tiles):
                    s1(i + ahead)
                t = tiles[i]
                norm_s2(xb_l[t % nbuf], ("xn", t % nbuf), dst_of(i), slot_of(i), bank_pool)

            return [(lambda i=i: step(i)) for i in range(len(tiles))]

        S.extra_reads = ["R_AB"]
        CV.off = base_off
        uT = CV.alloc((NCH, SEQ), BF16)
        wB = CV.alloc((3, NCH, 128), BF16)
        wza = CV.alloc((2, NCH, 128), BF16)
        qT2 = [CV.alloc((SEQ,), BF16) for _ in range(2)]
        kT2 = [CV.alloc((SEQ,), BF16) for _ in range(2)]
        vT = CV.alloc((SEQ,), BF16)
        vP = [CV.alloc((32, 128), BF16) for _ in range(3)]
        NPT = 12
        NSS = 6
        PT = [CV.alloc((2, 256), BF16) for _ in range(NPT)]
        rawb = [CV.alloc((2, 256), BF16) for _ in range(NSS)]
        hat = [CV.alloc((512,), BF16) for _ in range(2)]
        Eb = CV.alloc((3, 2, 256), BF16)
        accN = CV.alloc((SEQ,), F32)
        accD = CV.alloc((SEQ,), F32)
        xt = [accN[:, 1024 * i:1024 * (i + 1)] for i in range(4)]
        xn = [accD[:, 512 * i:512 * (i + 1)].bitcast(BF16) for i in range(4)]
        junkb = accD[:, 2048:2560].bitcast(BF16)
        tzb = [CV.alloc((512,), F32) for _ in range(2)]
        stg = [CV.alloc((256,), F32) for _ in range(2)]
        if os.environ.get("KVERB"):
            print("phase AB carve words", CV.off, "of", NW)

        for st_ in norm_steps(list(range(32)), 0, xt, xn, junkb,
                              lambda i: uT[:, :, i * 128:(i + 1) * 128], lambda i: ("uT", i),
                              (0, 1, 2, 3, 4, 5, 6, 7), ahead=2):
            st_()

        ATT_BANKS = (0, 1, 6, 7)
        PV_BANKS = (2, 3)
        PRJ_BANKS = (4, 5)
        NHP = 8 if _STOP >= 4 else (1 if _STOP >= 2 else 0)
        NHP = min(NHP, int(os.environ.get("KNHP", "8")))

        def load_wB(hp):
            for j in range(3):
                col0 = 4096 + j * 1024 + hp * 128
                DMA("pool", wB[:, j, :, :], w_in_v[:, :, col0:col0 + 128], [], [("wB", j)])

        def load_wza(hp):
            col0 = 4096 + 3 * 1024 + hp * 128
            DMA("pool", wza[:, hp % 2, :, :], w_in_v[:, :, col0:col0 + 128], [], [("wza", hp % 2)])

        def load_Eb(hp):
            DMA("sp", Eb, etab_d[hp].rearrange("p (a b c) -> p a b c", a=3, b=2), [], ["Eb"])

        uT_all = [("uT", t) for t in range(32)]
        uT_all2 = [(("uT", t), 1) for t in range(32)]
        prj_rr = [0]

        def proj_groups(hp, interleaved):
            par = hp % 2
            groups = []
            for j, dstT, nm in ((0, qT2[par], ("qT", par)), (1, kT2[par], ("kT", par)), (2, vT, ("vT",))):
                for tt in range(8):
                    def g(j=j, dstT=dstT, nm=nm, tt=tt):
                        if interleaved:
                            b = PRJ_BANKS[prj_rr[0] % 2]; prj_rr[0] += 1
                        else:
                            b = next_bank(ATT_BANKS + PRJ_BANKS)
                        for c in range(NCH):
                            MM(banks[b][:, :], wB[:, j, c, :], uT[:, c, tt * 512:(tt + 1) * 512], c == 0, c == NCH - 1,
                               [("wB", j)] + uT_all[tt * 4:tt * 4 + 4] + uT_all2[tt * 4:tt * 4 + 4], [bslot(b)])
                        dst = dstT[:, tt * 512:(tt + 1) * 512]
                        slot = nm + (tt,)
                        if interleaved:
                            def ev(j=j, dst=dst, b=b, slot=slot):
                                if j == 0:
                                    TS("dve", dst, banks[b][:, :], 0.125, 0.0, ALU.mult, ALU.add,
                                       [bslot(b), "PVDONE"], [slot, "GEVAC"])
                                else:
                                    COPY("dve", dst, banks[b][:, :], [bslot(b), "PVDONE"], [slot, "GEVAC"])
                            return ev
                        else:
                            if j == 0:
                                ACT(dst, banks[b][:, :], AF.Copy, [bslot(b)], [slot], scale=0.125)
                            elif j == 1:
                                COPY("dve", dst, banks[b][:, :], [bslot(b)], [slot])
                            else:
                                ACT(dst, banks[b][:, :], AF.Copy, [bslot(b)], [slot])
                        return slot
                    groups.append(g)
            return groups

        def emit_vtrans(hp):
            for pi, d in enumerate(PATTERNS):
                nb = SEQ // (128 * d)
                for g in range(4):
                    b = next_bank(ATT_BANKS)
                    pbf = banks[b][:, :].bitcast(BF16)
                    for k in range(8):
                        blk = g * 8 + k
                        r, j = blk // nb, blk % nb
                        s0 = 128 * j * d + r
                        TR(pbf[:, k * 128:(k + 1) * 128], vT[:, s0:s0 + 127 * d + 1:d], idb,
                           [("vT", tt) for tt in range(8)] + ["idb"], [bslot(b)])
                    COPY("dve", vP[pi][:, g * 8:(g + 1) * 8, :], pbf.rearrange("p (b f) -> p b f", b=8),
                         [bslot(b)], [("vP", pi, g)])

        rr = {"ss": 0, "pt": 0, "stg": 0}

        def emit_S(u, par):
            pi, d, nb, r, jp = u
            qT = qT2[par]; kT = kT2[par]
            pts = []
            raws = []
            for a in range(2):
                b = next_bank(ATT_BANKS)
                for jj in range(2):
                    j = 2 * jp + jj
                    nq = 256 if j + 1 < nb else 128
                    s0 = 128 * j * d + r
                    lo_t = s0 // 512
                    rd_q = [("qT", par, tt) for tt in range(lo_t, min(7, (s0 + (nq - 1) * d) // 512) + 1)]
                    rd_k = [("kT", par, tt) for tt in range(lo_t, min(7, (s0 + 127 * d) // 512) + 1)]
                    MM(banks[b][:, jj * 256:jj * 256 + nq],
                       kT[64 * a:64 * a + 64, s0:s0 + 127 * d + 1:d],
                       qT[64 * a:64 * a + 64, s0:s0 + (nq - 1) * d + 1:d],
                       True, True, rd_q + rd_k, [bslot(b)])
                sb = rr["ss"] % NSS; rr["ss"] += 1
                pt = rr["pt"] % NPT; rr["pt"] += 1
                pts.append(pt)
                raws.append(("raw", sb))
                e_b = bass.AP(Eb.tensor, Eb.offset + (pi * 2 + a) * 256, [[Eb.ap[0][0], 128], [0, 2], [1, 256]])
                ACT(rawb[sb][:, :, :], banks[b][:, :].rearrange("p (k q) -> p k q", k=2), AF.Exp, [bslot(b)], [("raw", sb)])
                TT("pool", PT[pt][:, :, :], rawb[sb][:, :, :], e_b, ALU.mult,
                   [("raw", sb), "Eb"], [("PT", pt)])
            return pts + raws

        pvstate = {"n0": 0}

        def emit_PV(u, pts, prev, extra=()):
            pi, d, nb, r, jp = u
            extra = list(extra) + ["EVAC", "GEVAC"]
            for jj in range(2):
                j = 2 * jp + jj
                if j % 4 == 0:
                    pvstate["n0"] = j
                pvs = PV_BANKS; n0 = pvstate["n0"]
                col = (j % 4) * 128
                blk_c = r * nb + j
                for (pb, which) in ((pvs[0], "num"), (pvs[1], "den")):
                    for a in range(2):
                        outp = banks[pb][64 * a:64 * a + 64, col:col + 128]
                        pairs = []
                        if jj == 1 or prev is not None:
                            ptp = pts[a] if jj == 1 else prev[a]
                            kidx = 0 if jj == 1 else 1
                            lh = vP[pi][:, blk_c - 1, 64 * a:64 * a + 64] if which == "num" else ones2
                            pairs.append((lh, PT[ptp][:, kidx, 128:256], ("PT", ptp), ("vP", pi, (blk_c - 1) // 8)))
                        lh = vP[pi][:, blk_c, 64 * a:64 * a + 64] if which == "num" else ones2
                        pairs.append((lh, PT[pts[a]][:, jj, 0:128], ("PT", pts[a]), ("vP", pi, blk_c // 8)))
                        for ii, (lh_, rh_, s1_, s2_) in enumerate(pairs):
                            MM(outp, lh_, rh_, ii == 0, ii == len(pairs) - 1, [s1_, s2_, "ones2"] + extra,
                               [bslot(pb), "PVDONE"])
                if j % 4 == 3 or j == nb - 1:
                    cnt = j - n0 + 1
                    t0 = 128 * n0 * d + r
                    nel = cnt * 128
                    lo, hi = t0 // 512, (t0 + (nel - 1) * d) // 512
                    for (pb, acc, nm) in ((pvs[0], accN, "accN"), (pvs[1], accD, "accD")):
                        dst = acc[:, t0:t0 + (nel - 1) * d + 1:d]
                        sl = [(nm, tt) for tt in range(lo, hi + 1)]
                        if pi == 0:
                            COPY("dve", dst, banks[pb][:, 0:nel], [bslot(pb)], sl + ["EVAC"])
                        else:
                            TT("dve", dst, banks[pb][:, 0:nel], dst, ALU.add, [bslot(pb)] + sl, sl + ["EVAC"])

        def emit_attention(hp, groups):
            par = hp % 2
            units = []
            for pi, d in enumerate(PATTERNS[:_NPAT]):
                nb = SEQ // (128 * d)
                for r in range(d):
                    for jp in range(nb // 2):
                        units.append((pi, d, nb, r, jp))
            blocks = [list(range(i, min(i + _KBLK, len(units)))) for i in range(0, len(units), _KBLK)]
            groups = list(groups)
            ptsl = {}
            for ui in blocks[0]:
                ptsl[ui] = emit_S(units[ui], par)
            for bi, blk in enumerate(blocks):
                extra = []
                if bi + 1 < len(blocks):
                    for ui in blocks[bi + 1]:
                        ptsl[ui] = emit_S(units[ui], par)
                        extra += ptsl[ui][2:4]
                ev = groups.pop(0)() if groups else None
                fin_tail = None
                u0 = units[blk[0]]
                if _KFIN and u0[1] == 1 and _KBLK == 2:
                    fin_tail = finalize_tile(hp, u0[4] // 2)
                for ui in blk:
                    u = units[ui]
                    emit_PV(u, ptsl[ui][0:2], ptsl[ui - 1][0:2] if u[4] > 0 else None, extra)
                if ev is not None:
                    ev()
                if fin_tail is not None:
                    fin_tail()
            for g in groups:
                g()()

        def finalize_tile(hp, tt):
            b = PRJ_BANKS[prj_rr[0] % 2]; prj_rr[0] += 1
            for c in range(NCH):
                MM(banks[b][:, :], wza[:, hp % 2, c, :], uT[:, c, tt * 512:(tt + 1) * 512], c == 0, c == NCH - 1,
                   [("wza", hp % 2)] + uT_all[tt * 4:tt * 4 + 4] + uT_all2[tt * 4:tt * 4 + 4], [bslot(b)])

            def tail():
                k2 = tt % 2
                sl = [("accD", tt)]
                ACT(tzb[k2], banks[b][:, :], AF.Tanh, [bslot(b), "PVDONE"], [("tzb", k2), "GEVAC"], scale=0.5)
                STT(tzb[k2], tzb[k2], 1.0, banks[b][:, :], ALU.add, ALU.mult, [("tzb", k2), bslot(b), "PVDONE"],
                    [("tzb", k2), "GEVAC"])
                RECIP(accD[:, tt * 512:(tt + 1) * 512], accD[:, tt * 512:(tt + 1) * 512], sl, sl)
                TT("pool", accN[:, tt * 512:(tt + 1) * 512], accN[:, tt * 512:(tt + 1) * 512],
                   accD[:, tt * 512:(tt + 1) * 512], ALU.mult, [("accD", tt), ("accN", tt)], [("accN", tt)])
                TT("pool", hat[k2], accN[:, tt * 512:(tt + 1) * 512], tzb[k2], ALU.mult,
                   [("accN", tt), ("tzb", k2)], [("hat", k2)])
                DMA("sp", ha_scr[hp * 128:(hp + 1) * 128, tt * 512:(tt + 1) * 512], hat[k2],
                    [("hat", k2)], [("ha_scr", tt)])
            return tail

        def emit_finalize(hp):
            for tt in range(8):
                sl = [("accD", tt)]
                RECIP(accD[:, tt * 512:(tt + 1) * 512], accD[:, tt * 512:(tt + 1) * 512], sl, sl)
            for tt in range(8):
                b = next_bank(ATT_BANKS)
                for c in range(NCH):
                    MM(banks[b][:, :], wza[:, hp % 2, c, :], uT[:, c, tt * 512:(tt + 1) * 512], c == 0, c == NCH - 1,
                       [("wza", hp % 2)] + uT_all[tt * 4:tt * 4 + 4] + uT_all2[tt * 4:tt * 4 + 4], [bslot(b)])
                k2 = tt % 2
                ACT(tzb[k2], banks[b][:, :], AF.Tanh, [bslot(b)], [("tzb", k2)], scale=0.5)
                STT(tzb[k2], tzb[k2], 1.0, banks[b][:, :], ALU.add, ALU.mult, [("tzb", k2), bslot(b)], [("tzb", k2)])
                TT("pool", accN[:, tt * 512:(tt + 1) * 512], accN[:, tt * 512:(tt + 1) * 512],
                   accD[:, tt * 512:(tt + 1) * 512], ALU.mult, [("accD", tt), ("accN", tt)], [("accN", tt)])
                TT("pool", hat[k2], accN[:, tt * 512:(tt + 1) * 512], tzb[k2], ALU.mult,
                   [("accN", tt), ("tzb", k2)], [("hat", k2)])
                DMA("sp", ha_scr[hp * 128:(hp + 1) * 128, tt * 512:(tt + 1) * 512], hat[k2],
                    [("hat", k2)], [("ha_scr", tt)])

        if NHP:
            load_wB(0)
            load_wza(0)
            load_Eb(0)
            for g in proj_groups(0, False):
                g()
        for hp in range(NHP):
            if hp + 1 < NHP:
                load_wB(hp + 1)
                load_wza(hp + 1)
            emit_vtrans(hp)
            emit_attention(hp, proj_groups(hp + 1, True) if hp + 1 < NHP else [])
            if hp + 1 < NHP:
                load_Eb(hp + 1)
            if not (_KFIN and _KBLK == 2 and PATTERNS[-1] == 1 and _NPAT == 3):
                emit_finalize(hp)

        S.extra_reads = []
        S.add("dve", lambda e: e.memset(fence_t, 0.0), writes=["R_AB", "FENCE"])
        S.extra_reads = ["FENCE"]
        CV.off = base_off
        wOC = CV.alloc((NCH, DM), BF16)
        wOA = CV.alloc((NCH, DM), BF16)
        wO = CV.alloc((NCH, DM), BF16)
        wGC = CV.alloc((NCH, DM), BF16)
        wGA = CV.alloc((NCH, DM), BF16)
        wC = [CV.alloc((4, NCH, 128), BF16) for _ in range(2)]
        uTq = CV.alloc((NCH, 1024), BF16)
        hcq = CV.alloc((NCH, 1024), BF16)
        haq = CV.alloc((NCH, 512), BF16)
        mrg = CV.alloc((NCH, 512), BF16)
        junkb = CV.alloc((DM,), BF16)
        fgt = CV.alloc((DM,), F32)
        xt = [CV.alloc((DM,), F32) for _ in range(2)]
        xn = [CV.alloc((DM,), BF16) for _ in range(2)]
        hsb = [CV.alloc((DM,), F32) for _ in range(2)]
        u1 = [CV.alloc((514,), F32) for _ in range(2)]
        cgs = [CV.alloc((512,), F32) for _ in range(2)]
        cvb = [CV.alloc((512,), F32) for _ in range(2)]
        tzc = [CV.alloc((512,), F32) for _ in range(2)]
        g1b = [CV.alloc((512,), F32) for _ in range(2)]
        g2b = [CV.alloc((512,), F32) for _ in range(2)]
        m1b = [CV.alloc((512,), F32)] * 2
        m2b = [CV.alloc((512,), F32)] * 2
        if os.environ.get("KVERB"):
            print("phase C carve words", CV.off, "of", NW)

        DMA("sp", fgt, fg_d[:, :], [], ["fgt"])

        def load_wC(qi, cc):
            sl = (qi * 8 + cc) % 2
            for j in range(4):
                col0 = j * 1024 + cc * 128
                DMA("pool", wC[sl][:, j, :, :], w_in_v[:, :, col0:col0 + 128], [], [("wC", sl, j)])

        resident_q = []
        for (dst, src, nm) in ((wOC, w_oc, "wOC"), (wGC, None, "wGC"), (wOA, w_oa, "wOA"), (wGA, None, "wGA"), (wO, w_o, "wO")):
            for c in range(NCH):
                resident_q.append((dst, src, nm, c))

        def load_resident(n):
            for _ in range(n):
                if not resident_q:
                    return
                dst, src, nm, c = resident_q.pop(0)
                if src is None:
                    col0 = 8192 if nm == "wGC" else 9216
                    DMA("pool", dst[:, c, :], w_in[c * 128:(c + 1) * 128, col0:col0 + 1024], [], [(nm, c)])
                else:
                    DMA("pool", dst[:, c, :], src[c * 128:(c + 1) * 128, :], [], [(nm, c)])

        load_wC(0, 0)
        ALLB = (0, 1, 2, 3, 4, 5, 6, 7)

        def c1_steps(q_):
            return norm_steps([q_ * 8 + k for k in range(8)], 32, xt, xn, junkb,
                              lambda i: uTq[:, :, i * 128:(i + 1) * 128], lambda i: ("uTq", i), ALLB, ahead=1)

        def load_haq(T0_):
            DMA("sp", haq, ha_scr_v[:, :, T0_:T0_ + 512], [("ha_scr", T0_ // 512)], ["haq"])
        step = 0
        fin_box = [0]
        for qi in range(4 if _STOP >= 5 else 0):
            if qi == 0:
                for st_ in c1_steps(0):
                    st_()
            for cc in range(NCH):
                sl = (qi * 8 + cc) % 2
                if cc + 1 < NCH:
                    load_wC(qi, cc + 1)
                elif qi + 1 < 4:
                    load_wC(qi + 1, 0)
                for hf in range(2):
                    load_resident(3)
                    k2 = step % 2; step += 1
                    tsl = [("uTq", hf * 4 + i) for i in range(4)] + [(("uTq", hf * 4 + i), 1) for i in range(4)]
                    bk = {}
                    for j in (0, 2, 3, 1):
                        b = next_bank(ALLB); bk[j] = b
                        for c in range(NCH):
                            MM(banks[b][:, :], wC[sl][:, j, c, :], uTq[:, c, hf * 512:(hf + 1) * 512], c == 0, c == NCH - 1,
                               [("wC", sl, j)] + tsl, [bslot(b)])
                    ACT(cgs[k2], banks[bk[2]][:, :], AF.Copy, [bslot(bk[2])], [("cgs", k2)])
                    COPY("pool", u1[k2][:, 0:2], carry[:, cc, :], ["carry"], [("u1", k2)])
                    TT("dve", u1[k2][:, 2:514], banks[bk[0]][:, :], cgs[k2], ALU.mult,
                       [bslot(bk[0]), ("cgs", k2), ("u1", k2)], [("u1", k2)])
                    COPY("pool", carry[:, cc, :], u1[k2][:, 512:514], [("u1", k2)], ["carry"])
                    ACT(tzc[k2], banks[bk[3]][:, :], AF.Tanh, [bslot(bk[3])], [("tzc", k2)], scale=0.5)
                    STT(tzc[k2], tzc[k2], 1.0, banks[bk[3]][:, :], ALU.add, ALU.mult, [("tzc", k2), bslot(bk[3])], [("tzc", k2)])
                    STT(tzc[k2], banks[bk[1]][:, :], 0.5, tzc[k2], ALU.mult, ALU.mult,
                        [bslot(bk[1]), ("tzc", k2)], [("tzc", k2)])
                    w0 = vecs[:, 24 + cc:25 + cc]; w1 = vecs[:, 32 + cc:33 + cc]; w2 = vecs[:, 40 + cc:41 + cc]
                    TS("pool", cvb[k2], u1[k2][:, 0:512], w0, 0.0, ALU.mult, ALU.add, [("u1", k2), "vecs"], [("cvb", k2)])
                    STT(cvb[k2], u1[k2][:, 1:513], w1, cvb[k2], ALU.mult, ALU.add, [("u1", k2), ("cvb", k2), "vecs"], [("cvb", k2)])
                    STT(cvb[k2], u1[k2][:, 2:514], w2, cvb[k2], ALU.mult, ALU.add, [("u1", k2), ("cvb", k2), "vecs"], [("cvb", k2)])
                    TT("pool", hcq[:, cc, hf * 512:(hf + 1) * 512], tzc[k2], cvb[k2], ALU.mult,
                       [("tzc", k2), ("cvb", k2)], [("hcq", cc, hf)])
            load_resident(100)
            for hf in range(2):
                T0 = qi * 1024 + hf * 512
                if qi == 0 and hf == 0:
                    load_haq(T0)
                tsl = [("uTq", hf * 4 + i) for i in range(4)] + [(("uTq", hf * 4 + i), 1) for i in range(4)]
                hsl = [("hcq", c, hf) for c in range(NCH)]
                for dc in range(NCH):
                    k2 = dc % 2
                    bk = [next_bank(ALLB) for _ in range(4)]
                    specs = ((wOC, "wOC", hcq[:, :, hf * 512:(hf + 1) * 512], hsl),
                             (wGC, "wGC", uTq[:, :, hf * 512:(hf + 1) * 512], tsl),
                             (wOA, "wOA", haq, ["haq"]),
                             (wGA, "wGA", uTq[:, :, hf * 512:(hf + 1) * 512], tsl))
                    for i, (wt, nm, rhs3, rsl) in enumerate(specs):
                        for c in range(NCH):
                            MM(banks[bk[i]][:, :], wt[:, c, dc * 128:(dc + 1) * 128], rhs3[:, c, :], c == 0, c == NCH - 1,
                               [(nm, c)] + rsl, [bslot(bk[i])])
                    ACT(g1b[k2], banks[bk[1]][:, :], AF.Tanh, [bslot(bk[1]), "hb"], [("g1b", k2)], scale=0.5, bias=hb[:, dc:dc + 1])
                    ACT(g2b[k2], banks[bk[3]][:, :], AF.Tanh, [bslot(bk[3]), "hb"], [("g2b", k2)], scale=0.5, bias=hb[:, 8 + dc:9 + dc])
                    TS("pool", g1b[k2], g1b[k2], 0.5, 0.5, ALU.mult, ALU.add, [("g1b", k2)], [("g1b", k2)])
                    TS("pool", g2b[k2], g2b[k2], 0.5, 0.5, ALU.mult, ALU.add, [("g2b", k2)], [("g2b", k2)])
                    TT("dve", m1b[0], banks[bk[0]][:, :], g1b[k2], ALU.mult, [bslot(bk[0]), ("g1b", k2)], [("m1b", 0)])
                    TT("dve", m2b[0], banks[bk[2]][:, :], g2b[k2], ALU.mult, [bslot(bk[2]), ("g2b", k2)], [("m2b", 0)])
                    TT("pool", mrg[:, dc, :], m1b[0], m2b[0], ALU.add, [("m1b", 0), ("m2b", 0)], [("mrg", dc)])
                msl = [("mrg", dc) for dc in range(NCH)]
                if hf == 0:
                    load_haq(T0 + 512)
                elif qi + 1 < 4:
                    load_haq(T0 + 512)

                def out_tk(tk, T0=T0, msl=msl):
                    nonlocal_fin = fin_box
                    t = T0 // 128 + tk
                    k2 = nonlocal_fin[0] % 2; nonlocal_fin[0] += 1
                    hs = [("hsb", k2, 0), ("hsb", k2, 1)]
                    DMA("sp", hsb[k2], x[t * 128:(t + 1) * 128, :], [], hs)
                    for nh in range(2):
                        b = next_bank(ALLB)
                        for c in range(NCH):
                            MM(banks[b][:, :], mrg[:, c, tk * 128:(tk + 1) * 128], wO[:, c, nh * 512:(nh + 1) * 512],
                               c == 0, c == NCH - 1, msl + [("wO", c)], [bslot(b)])
                        TT("dve", hsb[k2][:, nh * 512:(nh + 1) * 512], banks[b][:, :], hsb[k2][:, nh * 512:(nh + 1) * 512],
                           ALU.add, [bslot(b), ("hsb", k2, nh)], [("hsb", k2, nh)])
                    ACT(junkb, hsb[k2], AF.Square, hs, ["junkb", ("ss2", t)], accum_out=stats2[:, 0, t:t + 1])
                    ACT(stats2[:, 1, t:t + 1], stats2[:, 0, t:t + 1], AF.Sqrt, [("ss2", t)], [("rs2", t)], scale=1.0 / DM, bias=EPS)
                    RECIP(stats2[:, 2, t:t + 1], stats2[:, 1, t:t + 1], [("rs2", t)], [("rstd2", t)])
                    STT(hsb[k2], hsb[k2], stats2[:, 2, t:t + 1], fgt, ALU.mult, ALU.mult, hs + [("rstd2", t), "fgt"], hs)
                    DMA("sp", y[t * 128:(t + 1) * 128, :], hsb[k2], hs, [("y", t)])

                if hf == 1 and qi + 1 < 4:
                    nxt = c1_steps(qi + 1)
                    for i in range(8):
                        nxt[i]()
                        if i % 2 == 1:
                            out_tk(i // 2)
                else:
                    for tk in range(4):
                        out_tk(tk)

        S.emit(st)
    return nc


def _consts():
    slopes = np.exp2(-8.0 * np.arange(1, 17, dtype=np.float32) / np.float32(16)).astype(np.float32)
    ik = np.arange(128)[:, None]
    iq = np.arange(128)[None, :]
    tab = np.empty((8, 128, 3, 2, 256), np.float32)
    for h in range(16):
        for pi, d in enumerate(PATTERNS):
            cur_delta = (iq - ik)
            nxt_delta = (128 + iq - ik)
            cur = np.where(cur_delta >= 0, -slopes[h] * (d * cur_delta).astype(np.float32), np.float32(NEG))
            nxt = np.where(nxt_delta <= 128, -slopes[h] * (d * nxt_delta).astype(np.float32), np.float32(NEG))
            tab[h // 2, :, pi, h % 2, 0:128] = cur
            tab[h // 2, :, pi, h % 2, 128:256] = nxt
    etab = np.exp(tab.astype(np.float64)).astype(np.float32).astype(ml_dtypes.bfloat16)
    return etab.reshape(8, 128, 1536), np.eye(128, dtype=np.float32)


def kernel(x, norm_g, w_in, b_merge, conv_w, w_out_conv, w_out_attn, w_o, final_g):
    x = np.asarray(x, np.float32)
    n = 8
    etab, idn = _consts()
    vecs = np.concatenate([
        np.asarray(norm_g, np.float32).reshape(8, 128).T,
        np.asarray(b_merge, np.float32).reshape(16, 128).T,
        np.asarray(conv_w, np.float32).reshape(3, 8, 128).transpose(2, 0, 1).reshape(128, 24),
    ], axis=1)
    vecs = np.ascontiguousarray(vecs, dtype=np.float32)
    fg = np.ascontiguousarray(np.broadcast_to(np.asarray(final_g, np.float32)[None, :], (128, DM)))
    gn = np.ascontiguousarray(np.broadcast_to(np.asarray(norm_g, np.float32).reshape(1, DM), (128, DM)))
    shared = {
        "w_in": np.ascontiguousarray(np.asarray(w_in, np.float32)[0]),
        "w_oc": np.ascontiguousarray(np.asarray(w_out_conv, np.float32)[0]),
        "w_oa": np.ascontiguousarray(np.asarray(w_out_attn, np.float32)[0]),
        "w_o": np.ascontiguousarray(np.asarray(w_o, np.float32)[0]),
        "vecs": vecs, "fg": fg, "gn": gn, "idn": idn, "etab": etab,
    }
    nc = build_program()
    in_maps = [dict(shared, x=np.ascontiguousarray(x[i])) for i in range(n)]
    res = run_bass_kernel_spmd(nc, in_maps, core_ids=list(range(n)))
    return np.stack([np.asarray(r["y"], dtype=np.float32) for r in res.results], axis=0)
```
